# Optimizing a Trainium2 kernel written in Bass

```python
import jax
import jax.numpy as jnp
from jax import lax
import numpy as np

D_MODEL = 4096
BATCH = 1
SEQ = 8192
DEPTH = 4

HEAD_DIM = 128
N_META = 16
GRID_W = 64
BLOCK = 128
WINDOW = 128
RMS_EPS = 1e-6
NEG_INF = -1e30

D_FF = (3 * D_MODEL) // 2

LRU_WIDTH = D_MODEL // 2
LRU_BLOCKS = LRU_WIDTH // HEAD_DIM
LRU_BLOCK_DIM = LRU_WIDTH // LRU_BLOCKS
CONV_WIDTH = 4
CONV_PAD_LEFT = 2
LRU_C = 8.0

SWA_Q_HEADS = (D_MODEL // 2) // HEAD_DIM
SWA_KV_HEADS = SWA_Q_HEADS // 4
SWA_Q_DIM = SWA_Q_HEADS * HEAD_DIM
SWA_KV_DIM = SWA_KV_HEADS * HEAD_DIM

GA_Q_HEADS = D_MODEL // HEAD_DIM
GA_KV_HEADS = GA_Q_HEADS // 4
GA_Q_DIM = GA_Q_HEADS * HEAD_DIM
GA_KV_DIM = GA_KV_HEADS * HEAD_DIM
ROPE_BASE = 10000.0

AB_IN = 2 * LRU_WIDTH + SWA_Q_DIM + 2 * SWA_KV_DIM
AB_OUT = LRU_WIDTH + SWA_Q_DIM
C_IN = GA_Q_DIM + 2 * GA_KV_DIM
C_OUT = GA_Q_DIM
N_AB = (DEPTH + 1) // 2
N_C = DEPTH // 2
ATTN_SCALE = HEAD_DIM ** -0.5

kernel_name = "hybrid_rglru_swa_axial_macaron_encoder"


def rms_norm(x, g):
    xf = x.astype(jnp.float32)
    y = xf * lax.rsqrt(jnp.mean(xf * xf, axis=-1, keepdims=True) + RMS_EPS)
    return (y * g.astype(jnp.float32)).astype(x.dtype)


def swiglu(x, w_gu, w_down):
    gate, up = jnp.split(x @ w_gu, 2, axis=-1)
    return (jax.nn.silu(gate) * up) @ w_down


def depthwise_conv(u, w, b):
    t = u.shape[1]
    up = jnp.pad(u, ((0, 0), (CONV_PAD_LEFT, CONV_WIDTH - 1 - CONV_PAD_LEFT), (0, 0)))
    out = up[:, 0:t] * w[0]
    for k in range(1, CONV_WIDTH):
        out = out + up[:, k:k + t] * w[k]
    return out + b


def _linear_recurrence_combine(left, right):
    a_l, b_l = left
    a_r, b_r = right
    return a_l * a_r, a_r * b_l + b_r


def rglru_direction(u, w_a, b_a, w_x, b_x, lam, reverse):
    bsz, t, _ = u.shape
    ub = u.reshape(bsz, t, LRU_BLOCKS, LRU_BLOCK_DIM)
    gate_r = jnp.einsum('btnc,ncd->btnd', ub, w_a).reshape(bsz, t, LRU_WIDTH) + b_a
    gate_i = jnp.einsum('btnc,ncd->btnd', ub, w_x).reshape(bsz, t, LRU_WIDTH) + b_x
    r = jax.nn.sigmoid(gate_r.astype(jnp.float32))
    i = jax.nn.sigmoid(gate_i.astype(jnp.float32))
    log_a = -LRU_C * r * jax.nn.softplus(-lam.astype(jnp.float32))
    a = jnp.exp(log_a)
    b = jnp.sqrt(-jnp.expm1(2.0 * log_a)) * (i * u.astype(jnp.float32))
    _, h = lax.associative_scan(_linear_recurrence_combine, (a, b), reverse=reverse, axis=1)
    return h


def sink_attend(q, k, v, bias, sink):
    s = jnp.einsum('bjqhgd,bjkhd->bjhgqk', q, k).astype(jnp.float32) * ATTN_SCALE + bias
    sink_col = jnp.broadcast_to(sink.astype(jnp.float32)[:, :, None, None], s.shape[:-1] + (1,))
    p = jax.nn.softmax(jnp.concatenate([sink_col, s], axis=-1), axis=-1)[..., 1:]
    return jnp.einsum('bjhgqk,bjkhd->bjqhgd', p.astype(v.dtype), v)


def windowed_attention(q, k, v, sink):
    bsz, t, hq, d = q.shape
    hkv = k.shape[2]
    g = hq // hkv
    n = t - N_META
    nb = n // BLOCK
    slopes = jnp.asarray(2.0 ** (-8.0 * np.arange(1, hq + 1) / hq), jnp.float32).reshape(hkv, g)
    sink_g = sink.reshape(hkv, g)
    qm, qr = q[:, :N_META], q[:, N_META:]
    km, kr = k[:, :N_META], k[:, N_META:]
    vm, vr = v[:, :N_META], v[:, N_META:]

    def band(z):
        zp = jnp.pad(z, ((0, 0), (BLOCK, BLOCK), (0, 0), (0, 0))).reshape(bsz, nb + 2, BLOCK, hkv, d)
        return jnp.concatenate([zp[:, :-2], zp[:, 1:-1], zp[:, 2:]], axis=2)

    k_blk = jnp.concatenate([jnp.broadcast_to(km[:, None], (bsz, nb, N_META, hkv, d)), band(kr)], axis=2)
    v_blk = jnp.concatenate([jnp.broadcast_to(vm[:, None], (bsz, nb, N_META, hkv, d)), band(vr)], axis=2)
    qpos = jnp.arange(nb)[:, None] * BLOCK + jnp.arange(BLOCK)[None, :]
    kpos = jnp.arange(nb)[:, None] * BLOCK - BLOCK + jnp.arange(3 * BLOCK)[None, :]
    rel = jnp.abs(qpos[:, :, None] - kpos[:, None, :])
    ok = (rel <= WINDOW) & (kpos[:, None, :] >= 0) & (kpos[:, None, :] < n)
    local_bias = jnp.where(ok[:, None, None],
                           -slopes[None, :, :, None, None] * rel[:, None, None].astype(jnp.float32),
                           NEG_INF)
    meta_bias = jnp.zeros((nb, hkv, g, BLOCK, N_META), jnp.float32)
    bias_r = jnp.concatenate([meta_bias, local_bias], axis=-1)[None]
    o_real = sink_attend(qr.reshape(bsz, nb, BLOCK, hkv, g, d), k_blk, v_blk, bias_r, sink_g)

    k_m = jnp.concatenate([km, kr[:, :BLOCK]], axis=1)[:, None]
    v_m = jnp.concatenate([vm, vr[:, :BLOCK]], axis=1)[:, None]
    key_comb = jnp.arange(N_META + BLOCK)
    ok_m = jnp.abs(key_comb[None, :] - jnp.arange(N_META)[:, None]) <= WINDOW
    bias_m = jnp.where(ok_m, 0.0, NEG_INF).astype(jnp.float32)[None, None, None, None]
    o_meta = sink_attend(qm.reshape(bsz, 1, N_META, hkv, g, d), k_m, v_m, bias_m, sink_g)

    return jnp.concatenate([o_meta.reshape(bsz, N_META, hq * d), o_real.reshape(bsz, n, hq * d)], axis=1)


def dense_attend(qb, k, v):
    s = jnp.einsum('bqhgd,bkhd->bhgqk', qb, k).astype(jnp.float32) * ATTN_SCALE
    p = jax.nn.softmax(s, axis=-1)
    return jnp.einsum('bhgqk,bkhd->bqhgd', p.astype(v.dtype), v)


def rope_rotate(x, ang):
    x1, x2 = jnp.split(x, 2, axis=-1)
    c, s = jnp.cos(ang), jnp.sin(ang)
    return jnp.concatenate([x1 * c - x2 * s, x2 * c + x1 * s], axis=-1)


def apply_axial_rope(x, ang_row, ang_col):
    xr, xc = jnp.split(x.astype(jnp.float32), 2, axis=-1)
    out = jnp.concatenate([rope_rotate(xr, ang_row[None, :, None]),
                           rope_rotate(xc, ang_col[None, :, None])], axis=-1)
    return out.astype(x.dtype)


def axial_angles(rows):
    half = HEAD_DIM // 2
    row = jnp.concatenate([jnp.zeros((N_META,), jnp.int32), jnp.repeat(jnp.arange(rows, dtype=jnp.int32), GRID_W)])
    col = jnp.concatenate([jnp.zeros((N_META,), jnp.int32), jnp.tile(jnp.arange(GRID_W, dtype=jnp.int32), rows)])
    inv_freq = ROPE_BASE ** (-(jnp.arange(half // 2, dtype=jnp.float32) * 2.0 / half))
    return row.astype(jnp.float32)[:, None] * inv_freq, col.astype(jnp.float32)[:, None] * inv_freq


def mixer_ab(h, w_in, w_out, conv_w, conv_b, w_a, b_a, w_x, b_x, lam, sink):
    bsz, t, _ = h.shape
    proj = h @ w_in
    u, gate, q, k, v = jnp.split(proj, [LRU_WIDTH, 2 * LRU_WIDTH, 2 * LRU_WIDTH + SWA_Q_DIM,
                                        2 * LRU_WIDTH + SWA_Q_DIM + SWA_KV_DIM], axis=-1)
    u = depthwise_conv(u, conv_w, conv_b)
    y_lru = (rglru_direction(u, w_a[0], b_a[0], w_x[0], b_x[0], lam[0], False)
             + rglru_direction(u, w_a[1], b_a[1], w_x[1], b_x[1], lam[1], True))
    y_a = (y_lru * jax.nn.gelu(gate.astype(jnp.float32))).astype(h.dtype)
    y_b = windowed_attention(q.reshape(bsz, t, SWA_Q_HEADS, HEAD_DIM),
                             k.reshape(bsz, t, SWA_KV_HEADS, HEAD_DIM),
                             v.reshape(bsz, t, SWA_KV_HEADS, HEAD_DIM), sink)
    return jnp.concatenate([y_a, y_b.astype(h.dtype)], axis=-1) @ w_out


def mixer_c(h, w_in, w_out, q_norm, k_norm, ang_row, ang_col):
    bsz, t, _ = h.shape
    g = GA_Q_HEADS // GA_KV_HEADS
    n = t - N_META
    nb = n // BLOCK
    q, k, v = jnp.split(h @ w_in, [GA_Q_DIM, GA_Q_DIM + GA_KV_DIM], axis=-1)
    q = apply_axial_rope(rms_norm(q.reshape(bsz, t, GA_Q_HEADS, HEAD_DIM), q_norm), ang_row, ang_col)
    k = apply_axial_rope(rms_norm(k.reshape(bsz, t, GA_KV_HEADS, HEAD_DIM), k_norm), ang_row, ang_col)
    v = v.reshape(bsz, t, GA_KV_HEADS, HEAD_DIM)
    q = q.reshape(bsz, t, GA_KV_HEADS, g, HEAD_DIM)
    o_meta = dense_attend(q[:, :N_META], k, v)
    q_blocks = jnp.moveaxis(q[:, N_META:].reshape(bsz, nb, BLOCK, GA_KV_HEADS, g, HEAD_DIM), 1, 0)
    o_real = lax.map(lambda qb: dense_attend(qb, k, v), q_blocks)
    o_real = jnp.moveaxis(o_real, 0, 1).reshape(bsz, n, GA_Q_DIM)
    o = jnp.concatenate([o_meta.reshape(bsz, N_META, GA_Q_DIM), o_real], axis=1)
    return o @ w_out


def setup_inputs(seed: int = 0) -> dict:
    key = jax.random.key(seed)
    ks = jax.random.split(key, 32)
    f32 = jnp.float32

    def nrm(i, shape, scale):
        return jax.random.normal(ks[i], shape, f32) * scale

    def gain(i, shape):
        return 1.0 + 0.02 * jax.random.normal(ks[i], shape, f32)

    a0 = jax.random.uniform(ks[20], (N_AB, 2, LRU_WIDTH), f32, 0.9, 0.999)
    s0 = a0 ** (1.0 / LRU_C)
    lru_lambda = jnp.log(s0) - jnp.log1p(-s0)
    return {
        "x": nrm(0, (BATCH, SEQ, D_MODEL), 1.0),
        "meta_tokens": nrm(1, (N_META, D_MODEL), 1.0),
        "ffn1_norm": gain(2, (DEPTH, D_MODEL)),
        "ffn1_w_gu": nrm(3, (DEPTH, D_MODEL, 2 * D_FF), D_MODEL ** -0.5),
        "ffn1_w_down": nrm(4, (DEPTH, D_FF, D_MODEL), D_FF ** -0.5),
        "mix_norm": gain(5, (DEPTH, D_MODEL)),
        "ab_w_in": nrm(6, (N_AB, D_MODEL, AB_IN), D_MODEL ** -0.5),
        "ab_w_out": nrm(7, (N_AB, AB_OUT, D_MODEL), AB_OUT ** -0.5),
        "lru_conv_w": nrm(8, (N_AB, CONV_WIDTH, LRU_WIDTH), CONV_WIDTH ** -0.5),
        "lru_conv_b": nrm(9, (N_AB, LRU_WIDTH), 0.02),
        "lru_w_a": nrm(10, (N_AB, 2, LRU_BLOCKS, LRU_BLOCK_DIM, LRU_BLOCK_DIM), LRU_BLOCK_DIM ** -0.5),
        "lru_b_a": nrm(11, (N_AB, 2, LRU_WIDTH), 0.02),
        "lru_w_x": nrm(12, (N_AB, 2, LRU_BLOCKS, LRU_BLOCK_DIM, LRU_BLOCK_DIM), LRU_BLOCK_DIM ** -0.5),
        "lru_b_x": nrm(13, (N_AB, 2, LRU_WIDTH), 0.02),
        "lru_lambda": lru_lambda,
        "swa_sink": nrm(14, (N_AB, SWA_Q_HEADS), 0.5),
        "c_w_in": nrm(15, (N_C, D_MODEL, C_IN), D_MODEL ** -0.5),
        "c_w_out": nrm(16, (N_C, C_OUT, D_MODEL), C_OUT ** -0.5),
        "c_q_norm": gain(17, (N_C, HEAD_DIM)),
        "c_k_norm": gain(18, (N_C, HEAD_DIM)),
        "ffn2_norm": gain(19, (DEPTH, D_MODEL)),
        "ffn2_w_gu": nrm(21, (DEPTH, D_MODEL, 2 * D_FF), D_MODEL ** -0.5),
        "ffn2_w_down": nrm(22, (DEPTH, D_FF, D_MODEL), D_FF ** -0.5),
        "final_norm": gain(23, (D_MODEL,)),
    }


def reference(x, meta_tokens, ffn1_norm, ffn1_w_gu, ffn1_w_down, mix_norm, ab_w_in, ab_w_out,
              lru_conv_w, lru_conv_b, lru_w_a, lru_b_a, lru_w_x, lru_b_x, lru_lambda, swa_sink,
              c_w_in, c_w_out, c_q_norm, c_k_norm, ffn2_norm, ffn2_w_gu, ffn2_w_down, final_norm):
    bsz, n, d = x.shape
    rows = n // GRID_W
    ang_row, ang_col = axial_angles(rows)
    h = jnp.concatenate([jnp.broadcast_to(meta_tokens.astype(x.dtype)[None], (bsz, N_META, d)), x], axis=1)
    for layer in range(DEPTH):
        h = h + 0.5 * swiglu(rms_norm(h, ffn1_norm[layer]), ffn1_w_gu[layer], ffn1_w_down[layer])
        hn = rms_norm(h, mix_norm[layer])
        i = layer // 2
        if layer % 2 == 0:
            h = h + mixer_ab(hn, ab_w_in[i], ab_w_out[i], lru_conv_w[i], lru_conv_b[i], lru_w_a[i],
                             lru_b_a[i], lru_w_x[i], lru_b_x[i], lru_lambda[i], swa_sink[i])
        else:
            h = h + mixer_c(hn, c_w_in[i], c_w_out[i], c_q_norm[i], c_k_norm[i], ang_row, ang_col)
        h = h + 0.5 * swiglu(rms_norm(h, ffn2_norm[layer]), ffn2_w_gu[layer], ffn2_w_down[layer])
    return rms_norm(h[:, N_META:], final_norm)
```

```python
import contextlib
import numpy as np
import ml_dtypes
import concourse.bass as bass
import concourse.mybir as mybir
from concourse.bass_utils import run_bass_kernel_spmd

F32 = mybir.dt.float32
BF16 = mybir.dt.bfloat16
AF = mybir.ActivationFunctionType
ALU = mybir.AluOpType
AX = mybir.AxisListType

NCORES = 8
D = 4096
DFF = 6144
NMETA = 16
SEQ = 8192
TL = 1040
TALL = SEQ + NMETA
EPS = 1e-6
ENGS = ("pe", "act", "dve", "pool", "sp")


class Op:
    __slots__ = ("eng", "fn", "dma", "deps", "idx", "milestone", "mcount", "dsem", "dcount", "users")

    def __init__(self, eng, fn, dma):
        self.eng = eng
        self.fn = fn
        self.dma = dma
        self.deps = set()
        self.milestone = False
        self.mcount = 0
        self.dsem = None
        self.dcount = 0
        self.users = 0


class Prog:
    def __init__(self, nc, n_dma_sems=12):
        self.nc = nc
        self.ops = []
        self.last_w = {}
        self.readers = {}
        self.n_dma_sems = n_dma_sems

    def op(self, eng, fn, reads=(), writes=(), dma=False):
        o = Op(eng, fn, dma)
        o.idx = len(self.ops)
        for r in reads:
            w = self.last_w.get(r)
            if w is not None:
                o.deps.add(w)
        for w_ in writes:
            w = self.last_w.get(w_)
            if w is not None:
                o.deps.add(w)
            for rd in self.readers.get(w_, ()):
                o.deps.add(rd)
        for r in reads:
            self.readers.setdefault(r, []).append(o)
        for w_ in writes:
            self.last_w[w_] = o
            self.readers[w_] = []
        o.deps.discard(o)
        self.ops.append(o)
        return o

    def pe(self, fn, reads=(), writes=()):
        return self.op("pe", fn, reads, writes)

    def act(self, fn, reads=(), writes=()):
        return self.op("act", fn, reads, writes)

    def dve(self, fn, reads=(), writes=()):
        return self.op("dve", fn, reads, writes)

    def pool(self, fn, reads=(), writes=()):
        return self.op("pool", fn, reads, writes)

    def dma(self, q, out, in_, reads=(), writes=(), **kw):
        return self.op(q, lambda e: e.dma_start(out=out, in_=in_, **kw), reads, writes, dma=True)

    def emit(self, final_wait_ops=()):
        nc = self.nc
        ops = self.ops
        for o in ops:
            for d in o.deps:
                d.users += 1
        per_eng = {e: [] for e in ENGS}
        for o in ops:
            per_eng[o.eng].append(o)
        stack = contextlib.ExitStack()
        with stack:
            esem = {e: stack.enter_context(nc.semaphore("s_" + e)) for e in ENGS}
            dsems = {}
            for q in ENGS:
                if any(o.dma for o in per_eng[q]):
                    dsems[q] = [stack.enter_context(nc.semaphore("d_%s%d" % (q, i)))
                                for i in range(self.n_dma_sems)]
            dcnt = {}
            prev_on_sem = {}
            rr = {q: 0 for q in ENGS}
            ecnt = {e: 0 for e in ENGS}
            final_wait_ops = list(final_wait_ops)
            fset = set(final_wait_ops)
            for o in ops:
                if o.dma:
                    i = rr[o.eng] % self.n_dma_sems
                    rr[o.eng] += 1
                    s = dsems[o.eng][i]
                    o.dsem = s
                    dcnt[s] = dcnt.get(s, 0) + 16
                    o.dcount = dcnt[s]
                    p = prev_on_sem.get(s)
                    if p is not None:
                        o.deps.add(p)
                    prev_on_sem[s] = o
                elif o.users > 0 or o in fset:
                    o.milestone = True
                    ecnt[o.eng] += 1
                    o.mcount = ecnt[o.eng]
            self.max_counts = dict(ecnt)
            block = stack.enter_context(nc.Block())

            def make_stream(ename):
                lst = per_eng[ename]

                def body(e):
                    waited = {}
                    for o in lst:
                        need = {}
                        for d in o.deps:
                            if d.dma:
                                s, v = d.dsem, d.dcount
                            else:
                                if d.eng == ename and ename == "pe":
                                    continue
                                s, v = esem[d.eng], d.mcount
                            if waited.get(s, 0) >= v:
                                continue
                            if need.get(s, 0) < v:
                                need[s] = v
                        for s, v in need.items():
                            e.wait_ge(s, v)
                            waited[s] = v
                        ins = o.fn(e)
                        if o.dma:
                            ins.then_inc(o.dsem, 16)
                        elif o.milestone:
                            ins.then_inc(esem[ename], 1)
                    if ename == "sp":
                        for o in final_wait_ops:
                            if o.dma:
                                e.wait_ge(o.dsem, o.dcount)
                            else:
                                e.wait_ge(esem[o.eng], o.mcount)
                return body

            if per_eng["pe"]:
                block.tensor(make_stream("pe"))
            if per_eng["act"]:
                block.scalar(make_stream("act"))
            if per_eng["dve"]:
                block.vector(make_stream("dve"))
            if per_eng["pool"]:
                block.gpsimd(make_stream("pool"))
            block.sync(make_stream("sp"))


def _v3(ap2d, inner):
    return ap2d.rearrange("p (a b) -> p a b", b=inner)


WM = 528
HALVES = ((0, 512), (512, 528))


def build_ffn(pre_proj, final_norm):
    nc = bass.Bass("TRN2", target_bir_lowering=False)
    hT = nc.dram_tensor("hT", [D, TL], F32, kind="ExternalInput").ap()
    w_gu = nc.dram_tensor("w_gu", [D, 2 * DFF], F32, kind="ExternalInput").ap()
    w_dn = nc.dram_tensor("w_dn", [DFF, D], F32, kind="ExternalInput").ap()
    gin = nc.dram_tensor("g", [128, 32], F32, kind="ExternalInput").ap()
    if pre_proj:
        yT = nc.dram_tensor("yT", [D, TL], BF16, kind="ExternalInput").ap()
        w_o = nc.dram_tensor("w_o", [D, D], F32, kind="ExternalInput").ap()
        h2T = nc.dram_tensor("h2T", [D, TL], F32).ap()
    if final_norm:
        fgin = nc.dram_tensor("fg", [128, 32], F32, kind="ExternalInput").ap()
        h3T = nc.dram_tensor("h3T", [D, TL], F32).ap()
    oT = nc.dram_tensor("oT", [D, TL], F32, kind="ExternalOutput").ap()

    st = contextlib.ExitStack()
    with st:
        sb = lambda name, shape, dt: st.enter_context(nc.sbuf_tensor(name, shape, dt))
        xn = sb("xn", [128, 32 * WM], BF16)
        hid = sb("hid", [128, 48 * WM], BF16)
        wsl = [sb("wsl%d" % i, [128, 48 * 256], BF16) for i in range(4)]
        hbuf = [sb("hbuf%d" % i, [128, WM], F32) for i in range(4)]
        sqb = [sb("sqb%d" % i, [128, WM], BF16) for i in range(2)]
        rstd = sb("rstd", [128, WM], F32)
        tmp = sb("tmp", [128, WM], F32)
        sg = [sb("sg%d" % i, [128, WM], F32) for i in range(2)]
        ho = [sb("ho%d" % i, [128, WM], F32) for i in range(2)]
        ones = sb("ones", [128, 128], BF16)
        gsb = sb("gsb_sb", [128, 32], F32)
        fgsb = sb("fgsb", [128, 32], F32)
        acc = [st.enter_context(nc.psum_tensor("acc%d" % i, [128, 1024], F32)) for i in range(4)]

        P = Prog(nc)
        P.dve(lambda e: e.memset(ones[:], 1.0), writes=["ones"])
        P.dma("sp", gsb[:], gin[:, :], writes=["gsb"])
        if final_norm:
            P.dma("sp", fgsb[:], fgin[:, :], writes=["fgsb"])
        cnt = {"hb": 0, "sq": 0, "ws": 0, "sg": 0, "ho": 0, "acc": 0}
        outs = []

        def mm_group(a, wt, wcol, nk, rhs_t, W, extra_reads):
            def fn(e):
                ins = None
                for k in range(nk):
                    lhsT = wt[:, k * 256 + wcol:k * 256 + wcol + 128]
                    ins = e.matmul(acc[a][:, 0:512], lhsT, rhs_t[:, k * WM:k * WM + 512],
                                   start=(k == 0), stop=(k == nk - 1))
                    if W > 512:
                        ins = e.matmul(acc[a][:, 512:W], lhsT, rhs_t[:, k * WM + 512:k * WM + W],
                                       start=(k == 0), stop=(k == nk - 1))
                return ins
            return P.pe(fn, reads=list(extra_reads), writes=[("acc", a)])

        def load_w(slot, wsrc, c0, nk):
            src = wsrc[:, c0:c0 + 256].rearrange("(k p) c -> p k c", p=128)
            dst = _v3(wsl[slot][:, 0:nk * 256], 256)
            step = 8
            for k0 in range(0, nk, step):
                P.dma("pool", dst[:, k0:k0 + step, :], src[:, k0:k0 + step, :], writes=[("ws", slot)])

        def norm_stats(src, hi, c0, W, gkey):
            for k in range(32):
                s = cnt["hb"] % 4
                cnt["hb"] += 1
                q = cnt["sq"] % 2
                cnt["sq"] += 1
                P.dma("sp", hbuf[s][:, 0:W], src[k * 128:(k + 1) * 128, c0:c0 + W],
                      reads=[(gkey, k, hi)], writes=[("hb", s)])
                P.act(lambda e, s=s, q=q: e.activation(sqb[q][:, 0:W], hbuf[s][:, 0:W], AF.Square),
                      reads=[("hb", s)], writes=[("sq", q)])

                def fn(e, q=q, k=k):
                    ins = e.matmul(acc[0][:, 0:512], ones[:], sqb[q][:, 0:512], start=(k == 0), stop=(k == 31))
                    if W > 512:
                        ins = e.matmul(acc[0][:, 512:W], ones[:], sqb[q][:, 512:W], start=(k == 0), stop=(k == 31))
                    return ins
                P.pe(fn, reads=[("sq", q), "ones"], writes=[("acc", 0)])
            P.dve(lambda e: e.tensor_scalar(tmp[:, 0:W], acc[0][:, 0:W], 1.0 / D, EPS, ALU.mult, ALU.add),
                  reads=[("acc", 0)], writes=["tmp"])
            P.act(lambda e: e.activation(tmp[:, 0:W], tmp[:, 0:W], AF.Sqrt), reads=["tmp"], writes=["tmp"])
            P.dve(lambda e: e.reciprocal(rstd[:, 0:W], tmp[:, 0:W]), reads=["tmp"], writes=["rstd"])

        for hi, (c0, W) in enumerate(HALVES):
            hsrc, hkey = hT, "hT"
            if pre_proj:
                ybv = _v3(hid[:, 0:32 * WM], WM)
                P.dma("sp", ybv[:, :, 0:W], yT[:, c0:c0 + W].rearrange("(k p) t -> p k t", p=128),
                      writes=[("hid", f) for f in range(48)])
                for db in range(16):
                    slot = cnt["ws"] % 4
                    cnt["ws"] += 1
                    load_w(slot, w_o, db * 256, 32)
                    for j in range(2):
                        d = db * 2 + j
                        a = cnt["acc"] % 4
                        cnt["acc"] += 1
                        mm_group(a, wsl[slot], j * 128, 32, hid, W,
                                 [("ws", slot)] + [("hid", f) for f in range(32)])
                        s = cnt["hb"] % 4
                        cnt["hb"] += 1
                        P.dma("sp", hbuf[s][:, 0:W], hT[d * 128:(d + 1) * 128, c0:c0 + W], writes=[("hb", s)])
                        o_ = cnt["ho"] % 2
                        cnt["ho"] += 1
                        P.dve(lambda e, a=a, s=s, o_=o_: e.tensor_tensor(ho[o_][:, 0:W], acc[a][:, 0:W],
                                                                         hbuf[s][:, 0:W], ALU.add),
                              reads=[("acc", a), ("hb", s)], writes=[("ho", o_)])
                        P.dma("sp", h2T[d * 128:(d + 1) * 128, c0:c0 + W], ho[o_][:, 0:W],
                              reads=[("ho", o_)], writes=[("h2T", d, hi)])
                hsrc, hkey = h2T, "h2T"
            norm_stats(hsrc, hi, c0, W, hkey)
            for k in range(32):
                s = cnt["hb"] % 4
                cnt["hb"] += 1
                P.dma("sp", hbuf[s][:, 0:W], hsrc[k * 128:(k + 1) * 128, c0:c0 + W],
                      reads=[(hkey, k, hi)], writes=[("hb", s)])
                P.dve(lambda e, s=s, k=k: e.scalar_tensor_tensor(
                    xn[:, k * WM:k * WM + W], hbuf[s][:, 0:W], gsb[:, k:k + 1], rstd[:, 0:W], ALU.mult, ALU.mult),
                    reads=[("hb", s), "gsb", "rstd"], writes=[("xn", k)])
            xn_keys = [("xn", k) for k in range(32)]
            for fb in range(24):
                sl_g = cnt["ws"] % 4
                sl_u = (cnt["ws"] + 1) % 4
                cnt["ws"] += 2
                load_w(sl_g, w_gu, fb * 256, 32)
                load_w(sl_u, w_gu, DFF + fb * 256, 32)
                for j in range(2):
                    f = fb * 2 + j
                    ag = cnt["acc"] % 4
                    au = (cnt["acc"] + 1) % 4
                    cnt["acc"] += 2
                    mm_group(ag, wsl[sl_g], j * 128, 32, xn, W, [("ws", sl_g)] + xn_keys)
                    mm_group(au, wsl[sl_u], j * 128, 32, xn, W, [("ws", sl_u)] + xn_keys)
                    s = cnt["sg"] % 2
                    cnt["sg"] += 1
                    P.act(lambda e, s=s, ag=ag: e.activation(sg[s][:, 0:W], acc[ag][:, 0:W], AF.Silu),
                          reads=[("acc", ag)], writes=[("sg", s)])
                    P.dve(lambda e, s=s, au=au, f=f: e.tensor_tensor(hid[:, f * WM:f * WM + W], sg[s][:, 0:W],
                                                                     acc[au][:, 0:W], ALU.mult),
                          reads=[("sg", s), ("acc", au)], writes=[("hid", f)])
            hid_keys = [("hid", f) for f in range(48)]
            dst, dkey = (h3T, "h3T") if final_norm else (oT, "oT")
            for db in range(16):
                slot = cnt["ws"] % 4
                cnt["ws"] += 1
                load_w(slot, w_dn, db * 256, 48)
                for j in range(2):
                    d = db * 2 + j
                    a = cnt["acc"] % 4
                    cnt["acc"] += 1
                    mm_group(a, wsl[slot], j * 128, 48, hid, W, [("ws", slot)] + hid_keys)
                    s = cnt["hb"] % 4
                    cnt["hb"] += 1
                    P.dma("sp", hbuf[s][:, 0:W], hsrc[d * 128:(d + 1) * 128, c0:c0 + W],
                          reads=[(hkey, d, hi)], writes=[("hb", s)])
                    o_ = cnt["ho"] % 2
                    cnt["ho"] += 1
                    P.dve(lambda e, a=a, s=s, o_=o_: e.scalar_tensor_tensor(
                        ho[o_][:, 0:W], acc[a][:, 0:W], 0.5, hbuf[s][:, 0:W], ALU.mult, ALU.add),
                        reads=[("acc", a), ("hb", s)], writes=[("ho", o_)])
                    od = P.dma("sp", dst[d * 128:(d + 1) * 128, c0:c0 + W], ho[o_][:, 0:W],
                               reads=[("ho", o_)], writes=[(dkey, d, hi)])
                    if not final_norm:
                        outs.append(od)
            if final_norm:
                norm_stats(h3T, hi, c0, W, "h3T")
                for k in range(32):
                    s = cnt["hb"] % 4
                    cnt["hb"] += 1
                    P.dma("sp", hbuf[s][:, 0:W], h3T[k * 128:(k + 1) * 128, c0:c0 + W],
                          reads=[("h3T", k, hi)], writes=[("hb", s)])
                    o_ = cnt["ho"] % 2
                    cnt["ho"] += 1
                    P.dve(lambda e, s=s, k=k, o_=o_: e.scalar_tensor_tensor(
                        ho[o_][:, 0:W], hbuf[s][:, 0:W], fgsb[:, k:k + 1], rstd[:, 0:W], ALU.mult, ALU.mult),
                        reads=[("hb", s), "fgsb", "rstd"], writes=[("ho", o_)])
                    outs.append(P.dma("sp", oT[k * 128:(k + 1) * 128, c0:c0 + W], ho[o_][:, 0:W],
                                      reads=[("ho", o_)], writes=[("oT", k, hi)]))
        P.emit(final_wait_ops=outs)
    return nc


def _vec128(v):
    return np.ascontiguousarray(np.asarray(v, np.float32).reshape(32, 128).T)


_NC_CACHE = {}


def _get_nc(key, builder):
    if key not in _NC_CACHE:
        _NC_CACHE[key] = builder()
    return _NC_CACHE[key]


def run_ffn(hT_shards, w_gu, w_dn, g, yT_shards=None, w_o=None, fg=None, trace=False):
    pre = yT_shards is not None
    fin = fg is not None
    nc = _get_nc(("ffn", pre, fin), lambda: build_ffn(pre, fin))
    gl = _vec128(g)
    in_maps = []
    for c in range(NCORES):
        m = {"hT": hT_shards[c], "w_gu": w_gu, "w_dn": w_dn, "g": gl}
        if pre:
            m["yT"] = yT_shards[c]
            m["w_o"] = w_o
        if fin:
            m["fg"] = _vec128(fg)
        in_maps.append(m)
    res = run_bass_kernel_spmd(nc, in_maps, core_ids=list(range(NCORES)), trace=trace)
    return [r["oT"] for r in res.results], res


def mixer_tiles(nreal):
    return [(0, 16)] + [(16 + 256 * i, 256) for i in range(nreal // 256)]


class FrontEnd:
    def __init__(self, nc, st, P, hT, gsb, ones, psN, tag=""):
        sb = lambda name, shape, dt: st.enter_context(nc.sbuf_tensor(name + tag, shape, dt))
        self.P, self.hT, self.gsb, self.ones, self.psN = P, hT, gsb, ones, psN
        self.xn = sb("fxn", [128, 32 * 256], BF16)
        self.hbuf = [sb("fhb%d" % i, [128, 256], F32) for i in range(4)]
        self.sqb = [sb("fsq%d" % i, [128, 256], BF16) for i in range(2)]
        self.rstd = sb("frstd", [128, 256], F32)
        self.tmp = sb("ftmp", [128, 256], F32)
        self.c_hb = 0
        self.c_sq = 0

    def tile(self, t0, W):
        P, hT, ones, psN = self.P, self.hT, self.ones, self.psN
        xn, hbuf, sqb, rstd, tmp = self.xn, self.hbuf, self.sqb, self.rstd, self.tmp
        for k in range(32):
            s = self.c_hb % 4
            self.c_hb += 1
            q = self.c_sq % 2
            self.c_sq += 1
            P.dma("sp", hbuf[s][:, 0:W], hT[k * 128:(k + 1) * 128, t0:t0 + W], writes=[("fhb", s)])
            P.act(lambda e, s=s, q=q: e.activation(sqb[q][:, 0:W], hbuf[s][:, 0:W], AF.Square),
                  reads=[("fhb", s)], writes=[("fsq", q)])
            P.pe(lambda e, q=q, k=k: e.matmul(psN[:, 0:W], ones[:], sqb[q][:, 0:W], start=(k == 0), stop=(k == 31)),
                 reads=[("fsq", q), "ones"], writes=["psN"])
        P.dve(lambda e: e.tensor_scalar(tmp[:, 0:W], psN[:, 0:W], 1.0 / D, EPS, ALU.mult, ALU.add),
              reads=["psN"], writes=["ftmp"])
        P.act(lambda e: e.activation(tmp[:, 0:W], tmp[:, 0:W], AF.Sqrt), reads=["ftmp"], writes=["ftmp"])
        P.dve(lambda e: e.reciprocal(rstd[:, 0:W], tmp[:, 0:W]), reads=["ftmp"], writes=["frstd"])
        for k in range(32):
            s = self.c_hb % 4
            self.c_hb += 1
            P.dma("sp", hbuf[s][:, 0:W], hT[k * 128:(k + 1) * 128, t0:t0 + W], writes=[("fhb", s)])
            P.dve(lambda e, s=s, k=k: e.scalar_tensor_tensor(
                xn[:, k * 256:k * 256 + W], hbuf[s][:, 0:W], self.gsb[:, k:k + 1], rstd[:, 0:W], ALU.mult, ALU.mult),
                reads=[("fhb", s), "gsb", "frstd"], writes=[("fxn", k)])
        return [("fxn", k) for k in range(32)]


def load_wsb(P, wsb, wsrc, ncols, key="wsb"):
    src = wsrc.rearrange("(k p) c -> p k c", p=128)
    dst = _v3(wsb[:, 0:32 * ncols], ncols)
    for k0 in range(0, 32, 4):
        P.dma("pool", dst[:, k0:k0 + 4, :], src[:, k0:k0 + 4, :], writes=[key])


def build_mix_c(nreal):
    T = nreal + NMETA
    nq = 1 + nreal // 128
    scale = 128 ** -0.5
    nc = bass.Bass("TRN2", target_bir_lowering=False)
    hT = nc.dram_tensor("hT", [D, T], F32, kind="ExternalInput").ap()
    w = nc.dram_tensor("w", [D, 768], F32, kind="ExternalInput").ap()
    gin = nc.dram_tensor("g", [128, 32], F32, kind="ExternalInput").ap()
    gqk = nc.dram_tensor("gqk", [128, 2], F32, kind="ExternalInput").ap()
    cosT = nc.dram_tensor("cosT", [128, T], F32, kind="ExternalInput").ap()
    sinT = nc.dram_tensor("sinT", [128, T], F32, kind="ExternalInput").ap()
    permD = nc.dram_tensor("perm", [128, 128], BF16, kind="ExternalInput").ap()
    oC = nc.dram_tensor("oC", [512, T], BF16, kind="ExternalOutput").ap()
    st = contextlib.ExitStack()
    with st:
        sb = lambda name, shape, dt: st.enter_context(nc.sbuf_tensor(name, shape, dt))
        ps = [st.enter_context(nc.psum_tensor("ps%d" % i, [128, 512], F32)) for i in range(8)]
        wsb = sb("wsb", [128, 32 * 768], BF16)
        qT2 = sb("qT2", [128, nq * 512], BF16)
        kT = sb("kT", [128, T], BF16)
        vR = sb("vR", [128, nq * 128], BF16)
        ones = sb("ones", [128, 128], BF16)
        perm = sb("perm_sb", [128, 128], BF16)
        gsb = sb("gsb_sb", [128, 32], F32)
        gqs = sb("gqs", [128, 2], F32)
        cs = [sb("cs%d" % i, [128, 256], F32) for i in range(2)]
        sn = [sb("sn%d" % i, [128, 256], F32) for i in range(2)]
        sq2 = sb("sq2", [128, 256], BF16)
        t2 = sb("t2", [128, 256], F32)
        r2 = sb("r2", [128, 256], F32)
        qn = sb("qn", [128, 256], F32)
        qnb = sb("qnb", [128, 256], BF16)
        ta = sb("ta", [128, 256], F32)
        tb = sb("tb", [128, 256], F32)
        Pt = [sb("Pt%d" % i, [128, 512], BF16) for i in range(3)]
        osb = [sb("osb%d" % i, [128, 512], BF16) for i in range(2)]
        rl = sb("rl", [128, 512], F32)

        P = Prog(nc)
        P.dve(lambda e: e.memset(ones[:], 1.0), writes=["ones"])
        P.dve(lambda e: e.memset(qT2[:, 0:512], 0.0), writes=[("q", 0)])
        P.dma("sp", gsb[:], gin[:, :], writes=["gsb"])
        P.dma("sp", gqs[:], gqk[:, :], writes=["gqs"])
        P.dma("sp", perm[:], permD[:, :], writes=["perm"])
        load_wsb(P, wsb, w, 768)
        fe = FrontEnd(nc, st, P, hT, gsb, ones, ps[0])
        cnt = {"pp": 0, "cs": 0}
        for ti, (t0, W) in enumerate(mixer_tiles(nreal)):
            xk = fe.tile(t0, W)
            c_ = cnt["cs"] % 2
            cnt["cs"] += 1
            P.dma("sp", cs[c_][:, 0:W], cosT[:, t0:t0 + W], writes=[("cs", c_)])
            P.dma("sp", sn[c_][:, 0:W], sinT[:, t0:t0 + W], writes=[("sn", c_)])
            qi0 = 0 if ti == 0 else 1 + 2 * (ti - 1)
            for j in range(5):
                pp = ps[1 + cnt["pp"] % 2]
                ppk = ("ps", 1 + cnt["pp"] % 2)
                cnt["pp"] += 1

                def fn(e, j=j, pp=pp):
                    ins = None
                    for k in range(32):
                        ins = e.matmul(pp[:, 0:W], wsb[:, k * 768 + j * 128:k * 768 + (j + 1) * 128],
                                       fe.xn[:, k * 256:k * 256 + W], start=(k == 0), stop=(k == 31))
                    return ins
                P.pe(fn, reads=["wsb"] + xk, writes=[ppk])
                P.act(lambda e, pp=pp: e.activation(sq2[:, 0:W], pp[:, 0:W], AF.Square), reads=[ppk], writes=["sq2"])
                P.pe(lambda e: e.matmul(ps[3][:, 0:W], ones[:], sq2[:, 0:W], start=True, stop=True),
                     reads=["sq2", "ones"], writes=[("ps", 3)])
                P.dve(lambda e: e.tensor_scalar(t2[:, 0:W], ps[3][:, 0:W], 1.0 / 128, EPS, ALU.mult, ALU.add),
                      reads=[("ps", 3)], writes=["t2"])
                P.act(lambda e: e.activation(t2[:, 0:W], t2[:, 0:W], AF.Sqrt), reads=["t2"], writes=["t2"])
                P.dve(lambda e: e.reciprocal(r2[:, 0:W], t2[:, 0:W]), reads=["t2"], writes=["r2"])
                gc = 0 if j < 4 else 1
                P.dve(lambda e, pp=pp, gc=gc: e.scalar_tensor_tensor(qn[:, 0:W], pp[:, 0:W], gqs[:, gc:gc + 1],
                                                                     r2[:, 0:W], ALU.mult, ALU.mult),
                      reads=[ppk, "gqs", "r2"], writes=["qn"])
                P.act(lambda e: e.activation(qnb[:, 0:W], qn[:, 0:W], AF.Copy), reads=["qn"], writes=["qnb"])
                P.pe(lambda e: e.matmul(ps[4][:, 0:W], perm[:], qnb[:, 0:W], start=True, stop=True),
                     reads=["qnb", "perm"], writes=[("ps", 4)])
                P.dve(lambda e, c_=c_: e.tensor_tensor(ta[:, 0:W], qn[:, 0:W], cs[c_][:, 0:W], ALU.mult),
                      reads=["qn", ("cs", c_)], writes=["ta"])
                P.dve(lambda e, c_=c_: e.tensor_tensor(tb[:, 0:W], ps[4][:, 0:W], sn[c_][:, 0:W], ALU.mult),
                      reads=[("ps", 4), ("sn", c_)], writes=["tb"])
                if j < 4:
                    if ti == 0:
                        dst = qT2[:, j * 128:j * 128 + W]
                        a_, b_ = ta[:, 0:W], tb[:, 0:W]
                        wk = [("q", 0)]
                    else:
                        dst = qT2[:, qi0 * 512:(qi0 + 2) * 512].rearrange("p (a h b) -> p a h b", h=4, b=128)[:, :, j, :]
                        a_ = ta[:, 0:256].rearrange("p (a b) -> p a b", b=128)
                        b_ = tb[:, 0:256].rearrange("p (a b) -> p a b", b=128)
                        wk = [("q", qi0), ("q", qi0 + 1)]
                else:
                    dst = kT[:, t0:t0 + W]
                    a_, b_ = ta[:, 0:W], tb[:, 0:W]
                    wk = [("k", qi0)] + ([("k", qi0 + 1)] if ti > 0 else [])
                P.dve(lambda e, dst=dst, a_=a_, b_=b_: e.tensor_tensor(dst, a_, b_, ALU.add),
                      reads=["ta", "tb"], writes=wk)
            blocks = [(0, 16)] if ti == 0 else [(0, 128), (128, 128)]
            for bi, (off, m) in enumerate(blocks):
                qi = qi0 + bi

                def fnv(e, off=off, m=m):
                    ins = None
                    for k in range(32):
                        ins = e.matmul(ps[5][0:m, 0:128], fe.xn[:, k * 256 + off:k * 256 + off + m],
                                       wsb[:, k * 768 + 640:k * 768 + 768], start=(k == 0), stop=(k == 31))
                    return ins
                P.pe(fnv, reads=["wsb"] + xk, writes=[("ps", 5)])
                P.act(lambda e, m=m, qi=qi: e.activation(vR[0:m, qi * 128:(qi + 1) * 128], ps[5][0:m, 0:128], AF.Copy),
                      reads=[("ps", 5)], writes=[("v", qi)])
        outs = []
        oCv = oC.rearrange("(j d) t -> d j t", d=128)
        sc = 0
        for qi in range(nq):
            psO = ps[3 + qi % 2]
            psL = ps[5 + qi % 2]
            kO, kL = ("ps", 3 + qi % 2), ("ps", 5 + qi % 2)
            pend = None
            for step in range(nq + 1):
                if step < nq:
                    ki = step
                    kp = 16 if ki == 0 else 128
                    kc0 = 0 if ki == 0 else 16 + 128 * (ki - 1)
                    b = sc % 3
                    sc += 1
                    P.pe(lambda e, b=b, kp=kp, kc0=kc0, qi=qi: e.matmul(
                        ps[b][0:kp, 0:512], kT[:, kc0:kc0 + kp], qT2[:, qi * 512:(qi + 1) * 512], start=True, stop=True),
                        reads=[("k", ki), ("q", qi)], writes=[("ps", b)])
                    P.act(lambda e, b=b, kp=kp: e.activation(Pt[b][0:kp, :], ps[b][0:kp, 0:512], AF.Exp, scale=scale),
                          reads=[("ps", b)], writes=[("Pt", b)])
                    nxt = (ki, kp, b)
                else:
                    nxt = None
                if pend is not None:
                    ki_, kp_, b_ = pend

                    def fo(e, ki_=ki_, kp_=kp_, b_=b_, psO=psO, psL=psL):
                        e.matmul(psO[:, 0:512], vR[0:kp_, ki_ * 128:(ki_ + 1) * 128], Pt[b_][0:kp_, :],
                                 start=(ki_ == 0), stop=(ki_ == nq - 1))
                        return e.matmul(psL[:, 0:512], ones[0:kp_, :], Pt[b_][0:kp_, :],
                                        start=(ki_ == 0), stop=(ki_ == nq - 1))
                    P.pe(fo, reads=[("Pt", b_), ("v", ki_), "ones"], writes=[kO, kL])
                pend = nxt
            o_ = qi % 2
            P.dve(lambda e, psL=psL: e.reciprocal(rl[:], psL[:, 0:512]), reads=[kL], writes=["rl"])
            P.dve(lambda e, psO=psO, o_=o_: e.tensor_tensor(osb[o_][:], psO[:, 0:512], rl[:], ALU.mult),
                  reads=[kO, "rl"], writes=[("osb", o_)])
            ov = osb[o_][:, :].rearrange("p (j b) -> p j b", b=128)
            if qi == 0:
                outs.append(P.dma("sp", oCv[:, :, 0:16], ov[:, :, 0:16], reads=[("osb", o_)]))
            else:
                tq = 16 + 128 * (qi - 1)
                outs.append(P.dma("sp", oCv[:, :, tq:tq + 128], ov, reads=[("osb", o_)]))
        P.emit(final_wait_ops=outs)
    return nc


def rope_tables(nreal):
    T = nreal + NMETA
    half = 64
    n = np.arange(nreal)
    row = np.concatenate([np.zeros(NMETA), n // 64]).astype(np.float32)
    col = np.concatenate([np.zeros(NMETA), n % 64]).astype(np.float32)
    inv = (np.float32(10000.0) ** (-(np.arange(half // 2, dtype=np.float32) * np.float32(2.0) / np.float32(half)))).astype(np.float32)
    ar = (row[:, None] * inv[None, :]).astype(np.float32)
    ac = (col[:, None] * inv[None, :]).astype(np.float32)
    ang = np.concatenate([ar, ar, ac, ac], axis=1).T
    cosT = np.cos(ang).astype(np.float32)
    sinT = np.sin(ang).astype(np.float32)
    sign = np.ones((128, 1), np.float32)
    sign[0:32] = -1
    sign[64:96] = -1
    perm = np.zeros((128, 128), np.float32)
    for d in range(128):
        partner = d + 32 if (d % 64) < 32 else d - 32
        perm[partner, d] = 1.0
    return np.ascontiguousarray(cosT), np.ascontiguousarray(sinT * sign), perm.astype(ml_dtypes.bfloat16)


def run_mix_c(hT_full, w_in, g, q_norm, k_norm, nreal=SEQ):
    nc = _get_nc(("mixc", nreal), lambda: build_mix_c(nreal))
    cosT, sinT, perm = rope_tables(nreal)
    gl = _vec128(g)
    gqk = np.ascontiguousarray(np.stack([q_norm, k_norm], axis=1).astype(np.float32))
    in_maps = []
    for c in range(NCORES):
        wc = np.ascontiguousarray(np.concatenate(
            [w_in[:, 512 * c:512 * (c + 1)], w_in[:, 4096 + 128 * c:4096 + 128 * (c + 1)],
             w_in[:, 5120 + 128 * c:5120 + 128 * (c + 1)]], axis=1))
        in_maps.append({"hT": hT_full, "w": wc, "g": gl, "gqk": gqk, "cosT": cosT, "sinT": sinT, "perm": perm})
    res = run_bass_kernel_spmd(nc, in_maps, core_ids=list(range(NCORES)))
    return np.concatenate([r["oC"] for r in res.results], axis=0), res


def build_mix_lru(nreal):
    T = nreal + NMETA
    nc = bass.Bass("TRN2", target_bir_lowering=False)
    hT = nc.dram_tensor("hT", [D, T], F32, kind="ExternalInput").ap()
    w = nc.dram_tensor("w", [D, 512], F32, kind="ExternalInput").ap()
    gin = nc.dram_tensor("g", [128, 32], F32, kind="ExternalInput").ap()
    cwD = nc.dram_tensor("cw", [128, 8], F32, kind="ExternalInput").ap()
    cbD = nc.dram_tensor("cb", [128, 2], F32, kind="ExternalInput").ap()
    waD = nc.dram_tensor("wa", [128, 512], F32, kind="ExternalInput").ap()
    wxD = nc.dram_tensor("wx", [128, 512], F32, kind="ExternalInput").ap()
    baD = nc.dram_tensor("ba", [128, 4], F32, kind="ExternalInput").ap()
    bxD = nc.dram_tensor("bx", [128, 4], F32, kind="ExternalInput").ap()
    lamD = nc.dram_tensor("lam", [128, 4], F32, kind="ExternalInput").ap()
    yA = nc.dram_tensor("yA", [256, T], BF16, kind="ExternalOutput").ap()
    TS = 512
    stiles = [(s0, min(TS, T - s0)) for s0 in range(0, T, TS)]
    st = contextlib.ExitStack()
    with st:
        sb = lambda name, shape, dt: st.enter_context(nc.sbuf_tensor(name, shape, dt))
        ps = [st.enter_context(nc.psum_tensor("ps%d" % i, [128, 512], F32)) for i in range(8)]
        wsb = sb("wsb", [128, 32 * 512], BF16)
        up = sb("up", [128, T + 3], F32)
        ge = sb("ge", [128, T], BF16)
        hf = sb("hf", [128, T], F32)
        ones = sb("ones", [128, 128], BF16)
        gsb = sb("gsb_sb", [128, 32], F32)
        cw = sb("cw_sb", [128, 8], F32)
        cb = sb("cb_sb", [128, 2], F32)
        wa = sb("wa_sb", [128, 512], BF16)
        wx = sb("wx_sb", [128, 512], BF16)
        ba = sb("ba_sb", [128, 4], F32)
        bx = sb("bx_sb", [128, 4], F32)
        lam = sb("lam_sb", [128, 4], F32)
        nsp = sb("nsp", [128, 4], F32)
        uc = sb("uc", [128, TS], F32)
        ucb = sb("ucb", [128, TS], BF16)
        rr = sb("rr", [128, TS], F32)
        ii = sb("ii", [128, TS], F32)
        aa = sb("aa", [128, TS], F32)
        bb = sb("bb", [128, TS], F32)
        om = sb("om", [128, TS], F32)
        hb = [sb("hb%d" % i, [128, TS], F32) for i in range(2)]
        ysb = [sb("ysb%d" % i, [128, TS], BF16) for i in range(2)]

        P = Prog(nc)
        P.dve(lambda e: e.memset(ones[:], 1.0), writes=["ones"])
        for dst, src, key in ((gsb, gin, "gsb"), (cw, cwD, "cw"), (cb, cbD, "cb"), (ba, baD, "ba"),
                              (bx, bxD, "bx"), (lam, lamD, "lam")):
            P.dma("sp", dst[:], src[:, :], writes=[key])
        P.dma("pool", wa[:], waD[:, :], writes=["wa"])
        P.dma("pool", wx[:], wxD[:, :], writes=["wx"])
        load_wsb(P, wsb, w, 512)
        P.act(lambda e: e.activation(nsp[:], lam[:], AF.Exp, scale=-1.0), reads=["lam"], writes=["nsp"])
        P.dve(lambda e: e.tensor_scalar(nsp[:], nsp[:], 1.0, None, ALU.add), reads=["nsp"], writes=["nsp"])
        P.act(lambda e: e.activation(nsp[:], nsp[:], AF.Ln), reads=["nsp"], writes=["nsp"])
        P.dve(lambda e: e.tensor_scalar(nsp[:], nsp[:], -8.0, None, ALU.mult), reads=["nsp"], writes=["nsp"])
        fe = FrontEnd(nc, st, P, hT, gsb, ones, ps[0])
        outs = []
        for ch in range(2):
            P.dve(lambda e: e.memset(up[:, 0:2], 0.0), writes=["up"])
            P.dve(lambda e: e.memset(up[:, T + 2:T + 3], 0.0), writes=["up"])
            for ti, (t0, W) in enumerate(mixer_tiles(nreal)):
                xk = fe.tile(t0, W)
                for which in range(2):
                    pp = ps[1 + which]
                    ppk = ("ps", 1 + which)
                    col = which * 256 + ch * 128

                    def fn(e, pp=pp, col=col, W=W):
                        ins = None
                        for k in range(32):
                            ins = e.matmul(pp[:, 0:W], wsb[:, k * 512 + col:k * 512 + col + 128],
                                           fe.xn[:, k * 256:k * 256 + W], start=(k == 0), stop=(k == 31))
                        return ins
                    P.pe(fn, reads=["wsb"] + xk, writes=[ppk])
                    if which == 0:
                        P.act(lambda e, pp=pp, t0=t0, W=W: e.activation(up[:, 2 + t0:2 + t0 + W], pp[:, 0:W], AF.Copy),
                              reads=[ppk], writes=["up"])
                    else:
                        P.act(lambda e, pp=pp, t0=t0, W=W: e.activation(ge[:, t0:t0 + W], pp[:, 0:W], AF.Gelu),
                              reads=[ppk], writes=["ge"])

            def ab_tile(s0, Ws, dr, ch=ch):
                c0 = ch * 4
                P.dve(lambda e: e.tensor_scalar(uc[:, 0:Ws], up[:, s0:s0 + Ws], cw[:, c0:c0 + 1], cb[:, ch:ch + 1],
                                                ALU.mult, ALU.add), reads=["up", "cw", "cb"], writes=["uc"])
                for k in range(1, 4):
                    P.dve(lambda e, k=k: e.scalar_tensor_tensor(uc[:, 0:Ws], up[:, s0 + k:s0 + k + Ws],
                                                                cw[:, c0 + k:c0 + k + 1], uc[:, 0:Ws], ALU.mult, ALU.add),
                          reads=["up", "cw", "uc"], writes=["uc"])
                P.act(lambda e: e.activation(ucb[:, 0:Ws], uc[:, 0:Ws], AF.Copy), reads=["uc"], writes=["ucb"])
                wc = (dr * 2 + ch) * 128
                bc = dr * 2 + ch
                P.pe(lambda e: e.matmul(ps[3][:, 0:Ws], wa[:, wc:wc + 128], ucb[:, 0:Ws], start=True, stop=True),
                     reads=["wa", "ucb"], writes=[("ps", 3)])
                P.pe(lambda e: e.matmul(ps[4][:, 0:Ws], wx[:, wc:wc + 128], ucb[:, 0:Ws], start=True, stop=True),
                     reads=["wx", "ucb"], writes=[("ps", 4)])
                P.act(lambda e: e.activation(rr[:, 0:Ws], ps[3][:, 0:Ws], AF.Sigmoid, bias=ba[:, bc:bc + 1]),
                      reads=[("ps", 3), "ba"], writes=["rr"])
                P.act(lambda e: e.activation(ii[:, 0:Ws], ps[4][:, 0:Ws], AF.Sigmoid, bias=bx[:, bc:bc + 1]),
                      reads=[("ps", 4), "bx"], writes=["ii"])
                P.act(lambda e: e.activation(aa[:, 0:Ws], rr[:, 0:Ws], AF.Exp, scale=nsp[:, bc:bc + 1]),
                      reads=["rr", "nsp"], writes=["aa"])
                P.dve(lambda e: e.tensor_tensor(om[:, 0:Ws], aa[:, 0:Ws], aa[:, 0:Ws], ALU.mult),
                      reads=["aa"], writes=["om"])
                P.dve(lambda e: e.tensor_scalar(om[:, 0:Ws], om[:, 0:Ws], -1.0, 1.0, ALU.mult, ALU.add),
                      reads=["om"], writes=["om"])
                P.dve(lambda e: e.tensor_scalar(om[:, 0:Ws], om[:, 0:Ws], 0.0, None, ALU.max), reads=["om"], writes=["om"])
                P.act(lambda e: e.activation(om[:, 0:Ws], om[:, 0:Ws], AF.Sqrt), reads=["om"], writes=["om"])
                P.dve(lambda e: e.tensor_tensor(ii[:, 0:Ws], ii[:, 0:Ws], uc[:, 0:Ws], ALU.mult),
                      reads=["ii", "uc"], writes=["ii"])
                P.dve(lambda e: e.tensor_tensor(bb[:, 0:Ws], om[:, 0:Ws], ii[:, 0:Ws], ALU.mult),
                      reads=["om", "ii"], writes=["bb"])

            for si, (s0, Ws) in enumerate(stiles):
                ab_tile(s0, Ws, 0)
                if si > 0:
                    P.dve(lambda e, s0=s0: e.scalar_tensor_tensor(bb[:, 0:1], aa[:, 0:1], hf[:, s0 - 1:s0], bb[:, 0:1],
                                                           ALU.mult, ALU.add),
                          reads=["aa", "bb", "hf"], writes=["bb"])
                P.dve(lambda e, s0=s0, Ws=Ws: e.tensor_tensor_scan(hf[:, s0:s0 + Ws], aa[:, 0:Ws], bb[:, 0:Ws], 0.0,
                                                     ALU.mult, ALU.add),
                      reads=["aa", "bb", "hf"], writes=["hf"])
            nst = len(stiles)
            for idx, si in enumerate(range(nst - 1, -1, -1)):
                s0, Ws = stiles[si]
                ab_tile(s0, Ws, 1)
                cur = hb[idx % 2]
                prev = hb[(idx + 1) % 2]
                if idx > 0:
                    P.dve(lambda e, prev=prev, Ws=Ws: e.scalar_tensor_tensor(bb[:, Ws - 1:Ws], aa[:, Ws - 1:Ws], prev[:, 0:1],
                                                                      bb[:, Ws - 1:Ws], ALU.mult, ALU.add),
                          reads=["aa", "bb", ("hb", (idx + 1) % 2)], writes=["bb"])
                P.dve(lambda e, cur=cur, Ws=Ws: e.tensor_tensor_scan(
                    cur[:, 0:Ws][:, ::-1], aa[:, 0:Ws][:, ::-1], bb[:, 0:Ws][:, ::-1], 0.0, ALU.mult, ALU.add),
                    reads=["aa", "bb"], writes=[("hb", idx % 2)])
                P.dve(lambda e, cur=cur, s0=s0, Ws=Ws: e.tensor_tensor(om[:, 0:Ws], cur[:, 0:Ws], hf[:, s0:s0 + Ws], ALU.add),
                      reads=[("hb", idx % 2), "hf"], writes=["om"])
                o_ = idx % 2
                P.dve(lambda e, o_=o_, s0=s0, Ws=Ws: e.tensor_tensor(ysb[o_][:, 0:Ws], om[:, 0:Ws], ge[:, s0:s0 + Ws], ALU.mult),
                      reads=["om", "ge"], writes=[("ysb", o_)])
                outs.append(P.dma("sp", yA[ch * 128:(ch + 1) * 128, s0:s0 + Ws], ysb[o_][:, 0:Ws],
                                  reads=[("ysb", o_)]))
        P.emit(final_wait_ops=outs)
    return nc


def run_mix_lru(hT_full, w_in, g, conv_w, conv_b, w_a, b_a, w_x, b_x, lam, nreal=SEQ):
    nc = _get_nc(("lru", nreal), lambda: build_mix_lru(nreal))
    gl = _vec128(g)
    in_maps = []
    for c in range(NCORES):
        ch0 = 256 * c
        wc = np.ascontiguousarray(np.concatenate([w_in[:, ch0:ch0 + 256], w_in[:, 2048 + ch0:2048 + ch0 + 256]], axis=1))
        cw = np.ascontiguousarray(conv_w[:, ch0:ch0 + 256].reshape(4, 2, 128).transpose(2, 1, 0).reshape(128, 8))
        cb = np.ascontiguousarray(conv_b[ch0:ch0 + 256].reshape(2, 128).T)
        wa = np.ascontiguousarray(w_a[:, 2 * c:2 * c + 2].transpose(2, 0, 1, 3).reshape(128, 512))
        wx = np.ascontiguousarray(w_x[:, 2 * c:2 * c + 2].transpose(2, 0, 1, 3).reshape(128, 512))
        f4 = lambda v: np.ascontiguousarray(v[:, ch0:ch0 + 256].reshape(2, 2, 128).transpose(2, 0, 1).reshape(128, 4))
        in_maps.append({"hT": hT_full, "w": wc, "g": gl, "cw": cw, "cb": cb, "wa": wa, "wx": wx,
                        "ba": f4(b_a), "bx": f4(b_x), "lam": f4(lam)})
    res = run_bass_kernel_spmd(nc, in_maps, core_ids=list(range(NCORES)))
    return np.concatenate([r["yA"] for r in res.results], axis=0), res


def build_mix_swa(nreal):
    T = nreal + NMETA
    nrb = nreal // 128
    nq = 1 + nrb
    scale = 128 ** -0.5
    nc = bass.Bass("TRN2", target_bir_lowering=False)
    hT = nc.dram_tensor("hT", [D, T], F32, kind="ExternalInput").ap()
    w = nc.dram_tensor("w", [D, 512], F32, kind="ExternalInput").ap()
    gin = nc.dram_tensor("g", [128, 32], F32, kind="ExternalInput").ap()
    bID = nc.dram_tensor("bI", [128, 800], F32, kind="ExternalInput").ap()
    bFD = nc.dram_tensor("bF", [128, 544], F32, kind="ExternalInput").ap()
    bLD = nc.dram_tensor("bL", [128, 544], F32, kind="ExternalInput").ap()
    bMD = nc.dram_tensor("bM", [128, 144], F32, kind="ExternalInput").ap()
    skD = nc.dram_tensor("sk", [128, 2], F32, kind="ExternalInput").ap()
    idD = nc.dram_tensor("ident", [128, 128], BF16, kind="ExternalInput").ap()
    yB = nc.dram_tensor("yB", [256, T], BF16, kind="ExternalOutput").ap()
    st = contextlib.ExitStack()
    with st:
        sb = lambda name, shape, dt: st.enter_context(nc.sbuf_tensor(name, shape, dt))
        ps = [st.enter_context(nc.psum_tensor("ps%d" % i, [128, 512], F32)) for i in range(6)]
        psT = [st.enter_context(nc.psum_tensor("psT%d" % i, [128, 512], F32)) for i in range(2)]
        wsb = sb("wsb", [128, 32 * 512], BF16)
        qS = sb("qS", [128, 2 * T], BF16)
        kS = sb("kS", [128, T], BF16)
        vR = sb("vR", [128, nq * 128], BF16)
        ones = sb("ones", [128, 128], BF16)
        gsb = sb("gsb_sb", [128, 32], F32)
        bI = sb("bI_sb", [128, 800], F32)
        bF = sb("bF_sb", [128, 544], F32)
        bL = sb("bL_sb", [128, 544], F32)
        bM = sb("bM_sb", [128, 144], F32)
        sk = sb("sk_sb", [128, 2], F32)
        ident = sb("ident_sb", [128, 128], BF16)
        ss = sb("ss", [128, 400], F32)
        ee = sb("ee", [128, 400], F32)
        pn = sb("pn", [128, 400], BF16)
        mx = sb("mx", [128, 1], F32)
        ngm = sb("ngm", [128, 1], F32)
        rs = sb("rs", [128, 1], F32)
        es = sb("es", [128, 1], F32)
        rd = sb("rd", [128, 1], F32)
        pT = [sb("pT%d" % i, [128, 512], BF16) for i in range(2)]
        ysb = [sb("ysb%d" % i, [128, 128], BF16) for i in range(2)]

        P = Prog(nc)
        P.dve(lambda e: e.memset(ones[:], 1.0), writes=["ones"])
        for dst, src, key in ((gsb, gin, "gsb"), (bI, bID, "bI"), (bF, bFD, "bF"), (bL, bLD, "bL"),
                              (bM, bMD, "bM"), (sk, skD, "sk"), (ident, idD, "ident")):
            P.dma("sp", dst[:], src[:, :], writes=[key])
        load_wsb(P, wsb, w, 512)
        fe = FrontEnd(nc, st, P, hT, gsb, ones, ps[0])
        cpp = 0
        for ti, (t0, W) in enumerate(mixer_tiles(nreal)):
            xk = fe.tile(t0, W)
            qi0 = 0 if ti == 0 else 1 + 2 * (ti - 1)
            for j in range(3):
                pp = ps[1 + cpp % 2]
                ppk = ("ps", 1 + cpp % 2)
                cpp += 1

                def fn(e, j=j, pp=pp, W=W):
                    ins = None
                    for k in range(32):
                        ins = e.matmul(pp[:, 0:W], wsb[:, k * 512 + j * 128:k * 512 + (j + 1) * 128],
                                       fe.xn[:, k * 256:k * 256 + W], start=(k == 0), stop=(k == 31))
                    return ins
                P.pe(fn, reads=["wsb"] + xk, writes=[ppk])
                dst = qS[:, j * T + t0:j * T + t0 + W] if j < 2 else kS[:, t0:t0 + W]
                P.act(lambda e, pp=pp, dst=dst, W=W: e.activation(dst, pp[:, 0:W], AF.Copy),
                      reads=[ppk], writes=["qk"])
            blocks = [(0, 16)] if ti == 0 else [(0, 128), (128, 128)]
            for bi, (off, m) in enumerate(blocks):
                qi = qi0 + bi

                def fnv(e, off=off, m=m):
                    ins = None
                    for k in range(32):
                        ins = e.matmul(ps[3][0:m, 0:128], fe.xn[:, k * 256 + off:k * 256 + off + m],
                                       wsb[:, k * 512 + 384:k * 512 + 512], start=(k == 0), stop=(k == 31))
                    return ins
                P.pe(fnv, reads=["wsb"] + xk, writes=[("ps", 3)])
                P.act(lambda e, m=m, qi=qi: e.activation(vR[0:m, qi * 128:(qi + 1) * 128], ps[3][0:m, 0:128], AF.Copy),
                      reads=[("ps", 3)], writes=["v"])
        outs = []
        it = 0
        import os as _os
        STG = int(_os.environ.get('SWA_STG', '6'))
        for j in range(2):
            for qb in range(nq):
                if qb == 0:
                    qn_, tq = 16, 0
                    segs = [(0, 16, 0), (16, 128, 1)]
                    bias = bM[0:16, 0:144]
                else:
                    r = qb - 1
                    qn_, tq = 128, 16 + 128 * r
                    lo, hi = max(r - 1, 0), min(r + 1, nrb - 1)
                    segs = [(0, 16, 0)] + [(16 + 128 * b, 128, 1 + b) for b in range(lo, hi + 1)]
                    if r == 0:
                        bias = bF[:, j * 272:j * 272 + 272]
                    elif r == nrb - 1:
                        bias = bL[:, j * 272:j * 272 + 272]
                    else:
                        bias = bI[:, j * 400:j * 400 + 400]
                n = sum(s_[1] for s_ in segs)
                nb = n - 16
                kb0 = segs[1][0]
                pS = ps[4 + it % 2]
                kS_ = ("ps", 4 + it % 2)
                qcols = qS[:, j * T + tq:j * T + tq + qn_]

                def fs(e, pS=pS, qcols=qcols, qn_=qn_, nb=nb, kb0=kb0):
                    e.matmul(pS[0:qn_, 0:16], qcols, kS[:, 0:16], start=True, stop=True)
                    return e.matmul(pS[0:qn_, 16:16 + nb], qcols, kS[:, kb0:kb0 + nb], start=True, stop=True)
                if STG >= 1: P.pe(fs, reads=["qk"], writes=[kS_])
                if STG >= 1: P.dve(lambda e, pS=pS, qn_=qn_, n=n, bias=bias: e.scalar_tensor_tensor(
                    ss[0:qn_, 0:n], pS[0:qn_, 0:n], scale, bias, ALU.mult, ALU.add),
                    reads=[kS_, "bI", "bF", "bL", "bM"], writes=["ss"])
                if STG >= 2: P.dve(lambda e, qn_=qn_, n=n: e.tensor_reduce(mx[0:qn_, :], ss[0:qn_, 0:n], AX.X, ALU.max),
                      reads=["ss"], writes=["mx"])
                if STG >= 2: P.dve(lambda e, qn_=qn_, j=j: e.tensor_tensor(mx[0:qn_, :], mx[0:qn_, :], sk[0:qn_, j:j + 1], ALU.max),
                      reads=["mx", "sk"], writes=["mx"])
                if STG >= 2: P.dve(lambda e, qn_=qn_: e.tensor_scalar(ngm[0:qn_, :], mx[0:qn_, :], -1.0, None, ALU.mult),
                      reads=["mx"], writes=["ngm"])
                if STG >= 3: P.act(lambda e, qn_=qn_, n=n: e.activation(ee[0:qn_, 0:n], ss[0:qn_, 0:n], AF.Exp,
                                                           bias=ngm[0:qn_, :], accum_out=rs[0:qn_, :]),
                      reads=["ss", "ngm"], writes=["ee", "rs"])
                if STG >= 3: P.act(lambda e, qn_=qn_, j=j: e.activation(es[0:qn_, :], sk[0:qn_, j:j + 1], AF.Exp, bias=ngm[0:qn_, :]),
                      reads=["sk", "ngm"], writes=["es"])
                if STG >= 4: P.dve(lambda e, qn_=qn_: e.tensor_tensor(rd[0:qn_, :], rs[0:qn_, :], es[0:qn_, :], ALU.add),
                      reads=["rs", "es"], writes=["rd"])
                if STG >= 4: P.dve(lambda e, qn_=qn_: e.reciprocal(rd[0:qn_, :], rd[0:qn_, :]), reads=["rd"], writes=["rd"])
                if STG >= 4: P.dve(lambda e, qn_=qn_, n=n: e.tensor_scalar(pn[0:qn_, 0:n], ee[0:qn_, 0:n], rd[0:qn_, :], None, ALU.mult),
                      reads=["ee", "rd"], writes=["pn"])
                tb_ = it % 2
                col = 0
                for si_, (kc0_, kc, blk) in enumerate(segs if STG >= 5 else []):
                    P.pe(lambda e, tb_=tb_, si_=si_, col=col, kc=kc, qn_=qn_: e.matmul(
                        psT[tb_][0:kc, si_ * 128:si_ * 128 + qn_], pn[0:qn_, col:col + kc], ident[0:qn_, 0:qn_],
                        start=True, stop=True),
                        reads=["pn", "ident"], writes=[("psT", tb_)])
                    P.act(lambda e, tb_=tb_, si_=si_, kc=kc, qn_=qn_: e.activation(
                        pT[tb_][0:kc, si_ * 128:si_ * 128 + qn_], psT[tb_][0:kc, si_ * 128:si_ * 128 + qn_], AF.Copy),
                        reads=[("psT", tb_)], writes=[("pT", tb_, si_)])
                    col += kc
                pO = ps[it % 2 + 1]
                kO = ("ps", it % 2 + 1)

                def fo(e, segs=segs, tb_=tb_, pO=pO, qn_=qn_):
                    ins = None
                    for si_, (kc0_, kc, blk) in enumerate(segs):
                        ins = e.matmul(pO[:, 0:qn_], vR[0:kc, blk * 128:(blk + 1) * 128],
                                       pT[tb_][0:kc, si_ * 128:si_ * 128 + qn_],
                                       start=(si_ == 0), stop=(si_ == len(segs) - 1))
                    return ins
                if STG >= 6: P.pe(fo, reads=["v"] + [("pT", tb_, si_) for si_ in range(len(segs))], writes=[kO])
                o_ = it % 2
                if STG >= 6: P.act(lambda e, o_=o_, pO=pO, qn_=qn_: e.activation(ysb[o_][:, 0:qn_], pO[:, 0:qn_], AF.Copy),
                      reads=[kO], writes=[("ysb", o_)])
                outs.append(P.dma("sp", yB[j * 128:(j + 1) * 128, tq:tq + qn_], ysb[o_][:, 0:qn_], reads=[("ysb", o_)]))
                it += 1
        P.emit(final_wait_ops=outs)
    return nc


def swa_bias_tables(hq_pair):
    NEG = np.float32(-1e30)
    q = np.arange(128)[:, None]
    bI = np.zeros((128, 800), np.float32)
    bF = np.zeros((128, 544), np.float32)
    bL = np.zeros((128, 544), np.float32)
    for j, hq in enumerate(hq_pair):
        slope = np.float32(2.0 ** (-8.0 * (hq + 1) / 16))
        c = np.arange(384)[None, :]
        rel = np.abs((c - 128) - q)
        bI[:, j * 400 + 16:j * 400 + 400] = np.where(rel <= 128, -slope * rel.astype(np.float32), NEG)
        c = np.arange(256)[None, :]
        rel = np.abs(c - q)
        bF[:, j * 272 + 16:j * 272 + 272] = np.where(rel <= 128, -slope * rel.astype(np.float32), NEG)
        rel = np.abs((c - 128) - q)
        bL[:, j * 272 + 16:j * 272 + 272] = np.where(rel <= 128, -slope * rel.astype(np.float32), NEG)
    bM = np.zeros((128, 144), np.float32)
    kc = np.arange(144)[None, :]
    bM[:] = np.where(np.abs(kc - q) <= 128, 0.0, NEG)
    return bI, bF, bL, bM


def run_mix_swa(hT_full, w_in, g, sink, nreal=SEQ):
    nc = _get_nc(("swa", nreal), lambda: build_mix_swa(nreal))
    gl = _vec128(g)
    ident = np.eye(128, dtype=np.float32).astype(ml_dtypes.bfloat16)
    in_maps = []
    for c in range(NCORES):
        kv = c // 2
        wc = np.ascontiguousarray(np.concatenate(
            [w_in[:, 4096 + 256 * c:4096 + 256 * (c + 1)], w_in[:, 6144 + 128 * kv:6144 + 128 * (kv + 1)],
             w_in[:, 6656 + 128 * kv:6656 + 128 * (kv + 1)]], axis=1))
        bI, bF, bL, bM = swa_bias_tables((2 * c, 2 * c + 1))
        skc = np.ascontiguousarray(np.broadcast_to(np.asarray(sink[2 * c:2 * c + 2], np.float32)[None, :], (128, 2)))
        in_maps.append({"hT": hT_full, "w": wc, "g": gl, "bI": bI, "bF": bF, "bL": bL, "bM": bM, "sk": skc,
                        "ident": ident})
    res = run_bass_kernel_spmd(nc, in_maps, core_ids=list(range(NCORES)))
    return np.concatenate([r["yB"] for r in res.results], axis=0), res


def _assemble_full(shards):
    hT = np.empty((D, TALL), shards[0].dtype)
    hT[:, 0:NMETA] = shards[0][:, 1024:1040]
    for c in range(NCORES):
        hT[:, NMETA + c * 1024:NMETA + (c + 1) * 1024] = shards[c][:, 0:1024]
    return hT


def _token_shards(full):
    return [np.ascontiguousarray(np.concatenate(
        [full[:, NMETA + c * 1024:NMETA + (c + 1) * 1024], full[:, 0:NMETA]], axis=1)) for c in range(NCORES)]


def kernel(x, meta_tokens, ffn1_norm, ffn1_w_gu, ffn1_w_down, mix_norm, ab_w_in, ab_w_out,
           lru_conv_w, lru_conv_b, lru_w_a, lru_b_a, lru_w_x, lru_b_x, lru_lambda, swa_sink,
           c_w_in, c_w_out, c_q_norm, c_k_norm, ffn2_norm, ffn2_w_gu, ffn2_w_down, final_norm):
    A = lambda v: np.asarray(v, np.float32)
    x = A(x)[0]
    meta = A(meta_tokens)
    shards = [np.ascontiguousarray(np.concatenate([x[c * 1024:(c + 1) * 1024], meta], axis=0).T)
              for c in range(NCORES)]
    depth = 4
    for layer in range(depth):
        i = layer // 2
        shards, _ = run_ffn(shards, A(ffn1_w_gu[layer]), A(ffn1_w_down[layer]), A(ffn1_norm[layer]))
        hT_full = _assemble_full(shards)
        if layer % 2 == 0:
            yA, _ = run_mix_lru(hT_full, A(ab_w_in[i]), A(mix_norm[layer]), A(lru_conv_w[i]), A(lru_conv_b[i]),
                                A(lru_w_a[i]), A(lru_b_a[i]), A(lru_w_x[i]), A(lru_b_x[i]), A(lru_lambda[i]))
            yB, _ = run_mix_swa(hT_full, A(ab_w_in[i]), A(mix_norm[layer]), A(swa_sink[i]))
            y_full = np.concatenate([yA, yB], axis=0)
            w_o = A(ab_w_out[i])
        else:
            y_full, _ = run_mix_c(hT_full, A(c_w_in[i]), A(mix_norm[layer]), A(c_q_norm[i]), A(c_k_norm[i]))
            w_o = A(c_w_out[i])
        y_shards = _token_shards(y_full)
        shards, _ = run_ffn(shards, A(ffn2_w_gu[layer]), A(ffn2_w_down[layer]), A(ffn2_norm[layer]),
                            yT_shards=y_shards, w_o=w_o, fg=A(final_norm) if layer == depth - 1 else None)
    out = np.empty((1, SEQ, D), np.float32)
    for c in range(NCORES):
        out[0, c * 1024:(c + 1) * 1024, :] = shards[c][:, 0:1024].T
    return out
```
